# Optimizing a Trainium2 kernel written in Bass

```python
import jax, jax.numpy as jnp
from jax import lax
import numpy as np

D_MODEL = 2048
BATCH = 2
SEQ = 8192
DEPTH = 1

CTX_LEN = 256
GRID_W = 64
D_MIX = D_MODEL
D_RWKV = D_MIX // 2
D_POOL = D_MIX - D_RWKV
HEAD_DIM = 64
N_HEADS = D_RWKV // HEAD_DIM
N_DIR = 2
LORA_W = 64
LORA_A = 64
POOL_WINDOWS = (2, 4, 8, 16)
N_POOL_GROUPS = len(POOL_WINDOWS)
POOL_GROUP = D_POOL // N_POOL_GROUPS
N_SHIFT = 3 * D_RWKV + N_DIR * (LORA_W + LORA_A)
O_G_RWKV = N_SHIFT
O_POOL = O_G_RWKV + D_RWKV
O_G_POOL = O_POOL + D_POOL
D_IN = O_G_POOL + D_POOL
RMS_EPS = 1e-6
LNX_EPS = 64e-5

kernel_name = "hybrid_rwkv7_pool_parallel_heads"


def rms_norm(x, g):
    xf = x.astype(jnp.float32)
    y = xf * lax.rsqrt(jnp.mean(xf * xf, axis=-1, keepdims=True) + RMS_EPS)
    return (y * g.astype(jnp.float32)).astype(x.dtype)


def centred_token_shift(u, mu):
    zero = jnp.zeros_like(u[:, :1])
    u_prev = jnp.concatenate([zero, u[:, :-1]], axis=1)
    u_next = jnp.concatenate([u[:, 1:], zero], axis=1)
    return u + mu[0] * (u_prev - u) + mu[1] * (u_next - u)


def centred_pool_minus_self(u, window):
    n = u.shape[-2]
    uf = u.astype(jnp.float32)
    cs = jnp.cumsum(uf, axis=-2)
    cs = jnp.concatenate([jnp.zeros_like(cs[..., :1, :]), cs], axis=-2)
    pos = np.arange(n)
    lo = np.clip(pos - window // 2, 0, n - 1)
    hi = np.clip(pos + window // 2 - 1, 0, n - 1)
    cnt = (hi - lo + 1).astype(np.float32)
    s = jnp.take(cs, hi + 1, axis=-2) - jnp.take(cs, lo, axis=-2)
    return s / cnt[:, None] - uf


def pool_mixer(p, w_pool, pool_scale, grid):
    B, L, _ = p.shape
    u = p.reshape(B, L // GRID_W, GRID_W, D_POOL) if grid else p
    parts = [centred_pool_minus_self(u[..., g * POOL_GROUP:(g + 1) * POOL_GROUP], win)
             for g, win in enumerate(POOL_WINDOWS)]
    z = jnp.stack(parts, axis=-2).reshape(B, L, N_POOL_GROUPS, POOL_GROUP)
    y = jnp.einsum('blgc,gcd->blgd', z, w_pool.astype(jnp.float32)).reshape(B, L, D_POOL)
    return y * pool_scale.astype(jnp.float32)


def rwkv_prepare(s, w0, w_up, a0, a_up, k_k, k_a):
    B, L, _ = s.shape
    f32 = jnp.float32
    s = s.astype(f32)
    r = s[..., :D_RWKV]
    k = s[..., D_RWKV:2 * D_RWKV]
    v = s[..., 2 * D_RWKV:3 * D_RWKV]
    o = 3 * D_RWKV
    dw = s[..., o:o + N_DIR * LORA_W].reshape(B, L, N_DIR, LORA_W)
    o = o + N_DIR * LORA_W
    da = s[..., o:o + N_DIR * LORA_A].reshape(B, L, N_DIR, LORA_A)
    w_log = w0.astype(f32) + jnp.einsum('bldr,drc->bldc', jnp.tanh(dw), w_up.astype(f32))
    w_log = -jax.nn.softplus(-w_log) - 0.5
    decay = jnp.exp(-jnp.exp(w_log))
    a = jax.nn.sigmoid(a0.astype(f32) + jnp.einsum('bldr,drc->bldc', da, a_up.astype(f32)))
    kk = (k * k_k.astype(f32)).reshape(B, L, N_HEADS, HEAD_DIM)
    kk = kk * lax.rsqrt(jnp.maximum(jnp.sum(kk * kk, axis=-1, keepdims=True), 1e-24))
    k_dir = k[:, :, None, :] * (1.0 + (a - 1.0) * k_a.astype(f32))
    hd = lambda t: t.reshape(t.shape[:-1] + (N_HEADS, HEAD_DIM))
    b = kk[:, :, None] * hd(a)
    return hd(r), hd(v), kk, hd(decay), hd(k_dir), b


def _rwkv_step(S, inp):
    r, w, k, v, a, b = inp
    sa = jnp.einsum('dbhvk,dbhk->dbhv', S, a)
    S = S * w[..., None, :] + sa[..., :, None] * b[..., None, :] + v[..., :, None] * k[..., None, :]
    y = jnp.einsum('dbhvk,dbhk->dbhv', S, r)
    return S, y


def _dir_time_major(t):
    t = jnp.transpose(t, (1, 2, 0, 3, 4))
    return jnp.stack([t[:, 0], jnp.flip(t[:, 1], axis=0)], axis=1)


def _shared_time_major(t):
    t = jnp.transpose(t, (1, 0, 2, 3))
    return jnp.stack([t, jnp.flip(t, axis=0)], axis=1)


def bidir_scan(S0, r, v, kk, decay, k_dir, b):
    xs = (_shared_time_major(r), _dir_time_major(decay), _dir_time_major(k_dir),
          _shared_time_major(v), _shared_time_major(-kk), _dir_time_major(b))
    S_fin, y = lax.scan(_rwkv_step, S0, xs)
    y = y[:, 0] + jnp.flip(y[:, 1], axis=0)
    return S_fin, jnp.transpose(y, (1, 0, 2, 3))


def branch_output(p, y, r, k_dir, v, grid, r_k, lnx_g, lnx_b, w_pool, pool_scale, w_out, out_dtype):
    f32 = jnp.float32
    B, L = y.shape[:2]
    mu = jnp.mean(y, axis=-1, keepdims=True)
    var = jnp.mean(jnp.square(y - mu), axis=-1, keepdims=True)
    yn = ((y - mu) * lax.rsqrt(var + LNX_EPS)).reshape(B, L, D_RWKV)
    yn = yn * lnx_g.astype(f32) + lnx_b.astype(f32)
    bonus = jnp.sum(r * jnp.sum(k_dir, axis=2) * r_k.astype(f32), axis=-1, keepdims=True) * v
    out_rwkv = (yn + bonus.reshape(B, L, D_RWKV)) * jax.nn.silu(p[..., O_G_RWKV:O_POOL].astype(f32))
    out_pool = pool_mixer(p[..., O_POOL:O_G_POOL], w_pool, pool_scale, grid)
    out_pool = out_pool * jax.nn.silu(p[..., O_G_POOL:D_IN].astype(f32))
    mix = jnp.concatenate([out_rwkv, out_pool], axis=-1).astype(out_dtype)
    return mix @ w_out


def setup_inputs(seed: int = 0) -> dict:
    key = jax.random.key(seed)
    ks = jax.random.split(key, 24)
    f32 = jnp.float32
    nrm = lambda k, shape, s: s * jax.random.normal(k, shape, f32)
    return {
        "x": nrm(ks[0], (BATCH, SEQ, D_MODEL), 1.0),
        "c": nrm(ks[1], (BATCH, D_MODEL), 1.0),
        "ctx": nrm(ks[2], (BATCH, CTX_LEN, D_MODEL), 1.0),
        "c_ctx": nrm(ks[3], (D_MODEL,), 1.0),
        "norm_g": 1.0 + nrm(ks[4], (DEPTH, D_MODEL), 0.02),
        "w_ada": nrm(ks[5], (DEPTH, D_MODEL, 3 * D_MODEL), D_MODEL ** -0.5),
        "b_ada": nrm(ks[6], (DEPTH, 3 * D_MODEL), 0.02),
        "w_in": nrm(ks[7], (DEPTH, D_MODEL, D_IN), D_MODEL ** -0.5),
        "shift_mu": jax.random.uniform(ks[8], (DEPTH, 2, N_SHIFT), f32, 0.0, 0.5),
        "w0": jax.random.uniform(ks[9], (DEPTH, N_DIR, D_RWKV), f32, -7.0, -1.0),
        "w_up": nrm(ks[10], (DEPTH, N_DIR, LORA_W, D_RWKV), LORA_W ** -0.5),
        "a0": nrm(ks[11], (DEPTH, N_DIR, D_RWKV), 0.1),
        "a_up": nrm(ks[12], (DEPTH, N_DIR, LORA_A, D_RWKV), LORA_A ** -0.5),
        "k_k": 0.85 + nrm(ks[13], (DEPTH, D_RWKV), 0.02),
        "k_a": 1.0 + nrm(ks[14], (DEPTH, D_RWKV), 0.02),
        "r_k": nrm(ks[15], (DEPTH, N_HEADS, HEAD_DIM), 0.1),
        "lnx_g": 1.0 + nrm(ks[16], (DEPTH, D_RWKV), 0.02),
        "lnx_b": nrm(ks[17], (DEPTH, D_RWKV), 0.02),
        "w_pool": nrm(ks[18], (DEPTH, N_POOL_GROUPS, POOL_GROUP, POOL_GROUP), POOL_GROUP ** -0.5),
        "pool_scale": 1.0 + nrm(ks[19], (DEPTH, D_POOL), 0.02),
        "w_out": nrm(ks[20], (DEPTH, D_MIX, D_MODEL), D_MIX ** -0.5),
        "final_g": 1.0 + nrm(ks[21], (D_MODEL,), 0.02),
    }


def reference(x, c, ctx, c_ctx, norm_g, w_ada, b_ada, w_in, shift_mu, w0, w_up, a0, a_up,
              k_k, k_a, r_k, lnx_g, lnx_b, w_pool, pool_scale, w_out, final_g):
    B = x.shape[0]
    silu_c = jax.nn.silu(c)
    silu_cc = jax.nn.silu(c_ctx)
    for l in range(DEPTH):
        mod_lat = silu_c @ w_ada[l] + b_ada[l]
        mod_ctx = silu_cc @ w_ada[l] + b_ada[l]
        sh_l, sc_l, gt_l = jnp.split(mod_lat[:, None, :], 3, axis=-1)
        sh_c, sc_c, gt_c = jnp.split(mod_ctx, 3)
        h_l = rms_norm(x, norm_g[l]) * (1 + sc_l) + sh_l
        h_c = rms_norm(ctx, norm_g[l]) * (1 + sc_c) + sh_c
        p_l = h_l @ w_in[l]
        p_c = h_c @ w_in[l]
        lat = rwkv_prepare(centred_token_shift(p_l[..., :N_SHIFT], shift_mu[l]),
                           w0[l], w_up[l], a0[l], a_up[l], k_k[l], k_a[l])
        cx = rwkv_prepare(centred_token_shift(p_c[..., :N_SHIFT], shift_mu[l]),
                          w0[l], w_up[l], a0[l], a_up[l], k_k[l], k_a[l])
        S0 = jnp.zeros((N_DIR, B, N_HEADS, HEAD_DIM, HEAD_DIM), jnp.float32)
        S_ctx, y_c = bidir_scan(S0, *cx)
        _, y_l = bidir_scan(S_ctx, *lat)
        r_l, v_l, _, _, kd_l, _ = lat
        out_l = branch_output(p_l, y_l, r_l, kd_l, v_l, True, r_k[l], lnx_g[l], lnx_b[l],
                              w_pool[l], pool_scale[l], w_out[l], x.dtype)
        if l < DEPTH - 1:
            r_c, v_c, _, _, kd_c, _ = cx
            out_c = branch_output(p_c, y_c, r_c, kd_c, v_c, False, r_k[l], lnx_g[l], lnx_b[l],
                                  w_pool[l], pool_scale[l], w_out[l], ctx.dtype)
            ctx = ctx + gt_c * out_c
        x = x + gt_l * out_l
    return rms_norm(x, final_g)
```

```python
import os
import numpy as np
from contextlib import ExitStack
import concourse.bass as bass
import concourse.mybir as mybir
from concourse.bass_utils import run_bass_kernel_spmd

F32 = mybir.dt.float32
BF16 = mybir.dt.bfloat16
AF = mybir.ActivationFunctionType
ALU = mybir.AluOpType

D = 2048
SEG = 2048
CTX = 256
NT = 2308
NROW = 2432
LAT0 = 1
CTX0 = 2051
DIN = 6400
C = 128
NCHL = 16
NCH = 18
TOKT = [(0, 512), (512, 512), (1024, 512), (1536, 512), (2048, 260)]
CEXP = float(np.exp(-0.5))
SAME_ENGINE_WAITS = True


def cstart(c):
    return LAT0 + C * c if c < NCHL else CTX0 + C * (c - NCHL)


class Res:
    __slots__ = ("name", "w", "rs")

    def __init__(self, name=""):
        self.name = name
        self.w = None
        self.rs = {}


class Sched:
    NDMA = 64

    def __init__(self, nc, es):
        self.nc = nc
        self.eng = {"pe": nc.tensor, "dve": nc.vector, "act": nc.scalar, "pool": nc.gpsimd, "sp": nc.sync}
        self.sem = {e: es.enter_context(nc.semaphore("s_" + e)) for e in self.eng}
        self.cnt = {e: 0 for e in self.eng}
        self.dsem = [es.enter_context(nc.semaphore("d%d" % i)) for i in range(self.NDMA)]
        self.dcnt = [0] * self.NDMA
        self.dnext = 0
        self.seen = {e: {} for e in self.eng}
        self.ninst = 0
        self.dead = False

    def _wait(self, e, key, val):
        if self.dead:
            return
        if key == e and (e == "pe" or not SAME_ENGINE_WAITS):
            return
        if self.seen[e].get(key, 0) >= val:
            return
        sem = self.sem[key] if isinstance(key, str) else self.dsem[key]
        self.eng[e].wait_ge(sem, val)
        self.seen[e][key] = val
        self.ninst += 1

    def _deps(self, e, reads, writes):
        for r in reads:
            if r.w is not None:
                self._wait(e, *r.w)
        for w in writes:
            if w.w is not None:
                self._wait(e, *w.w)
            for k, v in w.rs.items():
                self._wait(e, k, v)

    def _commit(self, tok, reads, writes):
        for r in reads:
            if r.rs.get(tok[0], 0) < tok[1]:
                r.rs[tok[0]] = tok[1]
        for w in writes:
            w.w = tok
            w.rs = {}

    def barrier(self):
        for e in self.eng:
            for f in self.eng:
                if f != e and self.cnt[f] > 0:
                    self._wait(e, f, self.cnt[f])
            for i in range(self.NDMA):
                if self.dcnt[i] > 0:
                    self._wait(e, i, self.dcnt[i])

    def op(self, e, fn, reads=(), writes=()):
        if self.dead:
            return None
        self._deps(e, reads, writes)
        inst = fn(self.eng[e])
        self.cnt[e] += 1
        inst.then_inc(self.sem[e], 1)
        self._commit((e, self.cnt[e]), reads, writes)
        self.ninst += 1
        return inst

    def dma(self, e, out, in_, reads=(), writes=(), **kw):
        if self.dead:
            return (0, 0)
        i = self.dnext
        self.dnext = (self.dnext + 1) % self.NDMA
        if self.dcnt[i] > 0:
            self._wait(e, i, self.dcnt[i])
        self._deps(e, reads, writes)
        inst = self.eng[e].dma_start(out=out, in_=in_, **kw)
        self.dcnt[i] += 16
        inst.then_inc(self.dsem[i], 16)
        tok = (i, self.dcnt[i])
        self._commit(tok, reads, writes)
        self.ninst += 1
        return tok


VEC = {}
_off = 0
for _n, _w in [("norm_g", 16), ("b_ada", 48), ("mu", 52), ("w0", 16), ("a0", 16), ("k_k", 8), ("k_a", 8),
               ("r_k", 8), ("lnx_g", 8), ("lnx_b", 8), ("pool_scale", 8), ("cmask", 16)]:
    VEC[_n] = (_off, _w)
    _off += _w
NVEC = _off


def build(dbg=None):
    nc = bass.Bass("TRN2", target_bir_lowering=False)
    dt = lambda n, s, d=F32, k="ExternalInput": nc.dram_tensor(n, s, d, kind=k).ap()
    xin = dt("xin", [NROW, D])
    xres = dt("xres", [SEG, D])
    cT = dt("cT", [128, 16, 2])
    colmask = dt("colmask", [128, NT])
    vecs = dt("vecs", [128, NVEC])
    w_ada = dt("w_ada", [D, 3 * D])
    bgate = dt("bgate", [128, D])
    fgb = dt("fgb", [128, D])
    w_in = dt("w_in", [D, DIN])
    wup = dt("wup", [128, 1024])
    aup = dt("aup", [128, 1024])
    wpool = dt("wpool", [128, 4, 2, 256])
    mpoolT = dt("mpoolT", [128, 4, 128])
    w_out = dt("w_out", [D, D])
    consts = dt("consts", [128, 320 + 1280 + 1792])
    out = dt("out", [SEG, D], F32, "ExternalOutput")
    if dbg is not None:
        dbg_out = dt("dbg", list(dbg[1]), F32, "ExternalOutput")
    pscr = nc.dram_tensor("pscr", [34, 128, NT], F32)
    mixscr = nc.dram_tensor("mixscr", [16, 128, SEG], BF16)
    agin = [nc.dram_tensor("agin%d" % i, [128, 256], F32) for i in range(8)]
    agout = [nc.dram_tensor("agout%d" % i, [1024, 256], F32) for i in range(8)]

    with ExitStack() as top:
        S = Sched(nc, top)
        cc_sem = top.enter_context(nc.semaphore("cc"))
        cc_cnt = [0]

        def SB(es, name, shape, dtype=F32):
            return es.enter_context(nc.sbuf_tensor(name, shape, dtype)), Res(name)

        psf = []
        for i in range(7):
            psf.append((top.enter_context(nc.psum_tensor("psf%d" % i, [128, 512], F32)), Res("psf%d" % i)))
        pst, r_pst = top.enter_context(nc.psum_tensor("pst", [128, 1024], BF16)), Res("pst")
        psn = [0]

        def PS():
            t = psf[psn[0] % 7]
            psn[0] += 1
            return t

        def TS(e, o, i0, s1, s2, op0, op1, R, W):
            if op1 is None:
                S.op(e, lambda g: g.tensor_scalar(out=o, in0=i0, scalar1=s1, scalar2=None, op0=op0), R, W)
            else:
                S.op(e, lambda g: g.tensor_scalar(out=o, in0=i0, scalar1=s1, scalar2=s2, op0=op0, op1=op1), R, W)

        def TT(e, o, i0, i1, op, R, W):
            S.op(e, lambda g: g.tensor_tensor(out=o, in0=i0, in1=i1, op=op), R, W)

        def STT(o, i0, sc, i1, op0, op1, R, W):
            S.op("dve", lambda g: g.scalar_tensor_tensor(out=o, in0=i0, scalar=sc, in1=i1, op0=op0, op1=op1), R, W)

        def ACT(o, i, func, R, W, bias=None, scale=None):
            kw = {}
            if bias is not None:
                kw["bias"] = bias
            if scale is not None:
                kw["scale"] = scale
            S.op("act", lambda g: g.activation(out=o, in_=i, func=func, **kw), R, W)

        def CP(e, o, i, R, W):
            if e == "act":
                S.op("act", lambda g: g.copy(out=o, in_=i), R, W)
            else:
                S.op(e, lambda g: g.tensor_copy(out=o, in_=i), R, W)

        def MM(o, l, r, R, W, start=True, stop=True):
            S.op("pe", lambda g: g.matmul(out=o, lhsT=l, rhs=r, start=start, stop=stop), R, W)

        def TR(o, i, idn, R, W):
            S.op("pe", lambda g: g.transpose(out=o, in_=i, identity=idn), R, W)

        def DMA(o, i, R, W, e="sp"):
            return S.dma(e, o, i, R, W)

        done = [False]

        def dump(tile_ap, res):
            t = DMA(dbg_out, tile_ap, [res], [])
            S._wait("sp", *t)
            S.dead = True

        cst, r_cst = SB(top, "cst", [128, 320])
        vec, r_vec = SB(top, "vec", [128, NVEC])
        DMA(cst[:], consts[:, 0:320], [], [r_cst])
        DMA(vec[:], vecs, [], [r_vec])
        identb, r_idb = SB(top, "identb", [128, 128], BF16)
        CP("dve", identb[:], cst[:, 0:128], [r_cst], [r_idb])
        blk1 = cst[:, 128:256]
        idstk = cst[:, 256:320]
        def V(name, j=0, n=1):
            o, w = VEC[name]
            return vec[:, o + j: o + j + n]

        gtbc, r_gtbc = SB(top, "gtbc", [128, D])

        with ExitStack() as esA:
            hT, r_hT = SB(esA, "hT", [128, 16, NT], BF16)
            cmk, r_cmk = SB(esA, "cmk", [128, NT])
            DMA(cmk[:], colmask, [], [r_cmk])
            modv, r_modv = SB(esA, "modv", [128, 4, 16])

            with ExitStack() as es0:
                ct_, r_ct = SB(es0, "cTt", [128, 16, 2])
                DMA(ct_[:], cT, [], [r_ct])
                sct, r_sct = SB(es0, "sct", [128, 16, 2])
                ACT(sct[:], ct_[:], AF.Silu, [r_ct], [r_sct])
                modT, r_modT = SB(es0, "modT", [128, 32, 2])
                wst = [SB(es0, "wadst%d" % i, [128, 16, 128]) for i in range(2)]
                for cc in range(32):
                    wt, r_wt = wst[cc % 2]
                    DMA(wt[:], w_ada[:, cc * 128:(cc + 1) * 128].rearrange("(k p) c -> p k c", p=128), [], [r_wt])
                    ps, r_ps = PS()
                    for kc in range(16):
                        MM(ps[:, 0:2], wt[:, kc, :], sct[:, kc, :], [r_wt, r_sct], [r_ps], start=(kc == 0), stop=(kc == 15))
                    CP("dve", modT[:, cc, :], ps[:, 0:2], [r_ps], [r_modT])
                bo = VEC["b_ada"][0]
                for j in range(2):
                    TT("dve", modv[:, 1 + 2 * j, :], modT[:, 0:16, j], vec[:, bo:bo + 16], ALU.add, [r_modT, r_vec], [r_modv])
                    TT("dve", modv[:, 2 * j, :], modT[:, 16:32, j], vec[:, bo + 16:bo + 32], ALU.add, [r_modT, r_vec], [r_modv])
                    STT(modv[:, 2 * j, :], modv[:, 2 * j, :], 1.0, V("norm_g", 0, 16), ALU.add, ALU.mult, [r_modv, r_vec], [r_modv])
                scb, r_scb = SB(es0, "scb", [128, 16, 128])
                ones_, r_ones = SB(es0, "ones", [128, 128])
                S.op("dve", lambda g: g.memset(ones_[:], 1.0), [], [r_ones])
                for kc in range(16):
                    TS("dve", scb[:, kc, :], ones_[:], sct[:, kc, 0:1], None, ALU.mult, None, [r_ones, r_sct], [r_scb])
                bg, r_bg = SB(es0, "bg", [128, D])
                DMA(bg[:], bgate, [], [r_bg])
                wg = [SB(es0, "wg%d" % i, [128, 8, 512]) for i in range(2)]
                for q in range(4):
                    ps, r_ps = PS()
                    for hf in range(2):
                        wt, r_wt = wg[hf]
                        DMA(wt[:], w_ada[hf * 1024:(hf + 1) * 1024, 4096 + q * 512: 4096 + (q + 1) * 512].rearrange("(k p) c -> p k c", p=128), [], [r_wt])
                        for k8 in range(8):
                            kc = hf * 8 + k8
                            MM(ps[:, :], scb[:, kc, :], wt[:, k8, :], [r_wt, r_scb], [r_ps], start=(kc == 0), stop=(kc == 15))
                    TT("dve", gtbc[:, q * 512:(q + 1) * 512], ps[:, :], bg[:, q * 512:(q + 1) * 512], ALU.add, [r_ps, r_bg], [r_gtbc])
                S.barrier()
            if dbg is not None and dbg[0] == "mod":
                dump(modv[:].rearrange("p a b -> p (a b)"), r_modv)

            with ExitStack() as es1:
                xt = [SB(es1, "xt%d" % i, [128, D]) for i in range(2)]
                xb = [SB(es1, "xb%d" % i, [128, D], BF16) for i in range(2)]
                junk, r_junk = SB(es1, "junk", [128, D], BF16)
                st, r_st = SB(es1, "st", [128, 4])
                for ti in range(19):
                    x_, r_x = xt[ti % 2]
                    xb_, r_xb = xb[ti % 2]
                    DMA(x_[:], xin[ti * 128:(ti + 1) * 128, :], [], [r_x])
                    S.op("act", lambda g: g.activation(out=junk[:], in_=x_[:], func=AF.Square, accum_out=st[:, 0:1]), [r_x], [r_junk, r_st])
                    ACT(st[:, 1:2], st[:, 0:1], AF.Sqrt, [r_st], [r_st], bias=1e-6, scale=1.0 / D)
                    S.op("dve", lambda g: g.reciprocal(out=st[:, 2:3], in_=st[:, 1:2]), [r_st], [r_st])
                    TS("dve", xb_[:], x_[:], st[:, 2:3], None, ALU.mult, None, [r_x, r_st], [r_xb])
                    ncol = 128 if ti < 18 else 4
                    c0 = ti * 128
                    for half in range(2):
                        for k8 in range(8):
                            kc = half * 8 + k8
                            TR(pst[:, k8 * 128:(k8 + 1) * 128], xb_[:, kc * 128:(kc + 1) * 128], identb[:], [r_xb, r_idb], [r_pst])
                        for k8 in range(8):
                            kc = half * 8 + k8
                            e = "dve" if k8 % 2 == 0 else "act"
                            if ti == 16:
                                segs = [(0, 3, 0), (3, 128, 1)]
                            elif ti > 16:
                                segs = [(0, ncol, 1)]
                            else:
                                segs = [(0, 128, 0)]
                            for (a, b_, j) in segs:
                                o = hT[:, kc, c0 + a:c0 + b_]
                                i = pst[:, k8 * 128 + a:k8 * 128 + b_]
                                if e == "dve":
                                    TS("dve", o, i, modv[:, 2 * j, kc:kc + 1], modv[:, 2 * j + 1, kc:kc + 1], ALU.mult, ALU.add, [r_pst, r_modv], [r_hT])
                                else:
                                    ACT(o, i, AF.Identity, [r_pst, r_modv], [r_hT], bias=modv[:, 2 * j + 1, kc:kc + 1], scale=modv[:, 2 * j, kc:kc + 1])
                S.barrier()
            if dbg is not None and dbg[0] == "hT":
                hdb, r_hdb = SB(esA, "hdb", [128, 2049])
                CP("dve", hdb[:], hT[:, dbg[2], 0:2049], [r_hT], [r_hdb])
                dump(hdb[:], r_hdb)

            with ExitStack() as es2:
                wf = [SB(es2, "wf%d" % i, [128, 16, 128]) for i in range(2)]
                wb = [SB(es2, "wb%d" % i, [128, 16, 128], BF16) for i in range(2)]
                pt = [SB(es2, "pt%d" % i, [128, NT]) for i in range(2)]
                es2a = ExitStack()
                t1, r_t1 = SB(es2a, "t1", [128, NT])
                s_ = [SB(es2a, "s%d" % i, [128, NT]) for i in range(2)]
                mu3, r_mu3 = SB(es2a, "mu3", [128, 26])
                mo = VEC["mu"][0]
                TT("dve", mu3[:], vec[:, mo:mo + 26], vec[:, mo + 26:mo + 52], ALU.add, [r_vec], [r_mu3])
                TS("dve", mu3[:], mu3[:], -1.0, 1.0, ALU.mult, ALU.add, [r_mu3], [r_mu3])
                cnt = 0

                def project(ct, dst, r_dst, c_lo, c_hi, i):
                    wf_, r_wf = wf[i % 2]
                    wb_, r_wb = wb[i % 2]
                    DMA(wf_[:], w_in[:, ct * 128:(ct + 1) * 128].rearrange("(k p) c -> p k c", p=128), [], [r_wf])
                    CP("act", wb_[:], wf_[:], [r_wf], [r_wb])
                    for (t0, tn) in TOKT:
                        a = max(t0, c_lo)
                        b_ = min(t0 + tn, c_hi)
                        if b_ <= a:
                            continue
                        ps, r_ps = PS()
                        for kc in range(16):
                            MM(ps[:, 0:b_ - a], wb_[:, kc, :], hT[:, kc, a:b_], [r_wb, r_hT], [r_ps], start=(kc == 0), stop=(kc == 15))
                        TT("dve", dst[:, a:b_], ps[:, 0:b_ - a], cmk[:, a:b_], ALU.mult, [r_ps, r_cmk], [r_dst])

                for ct in range(26):
                    p_, r_p = pt[ct % 2]
                    so, r_so = s_[ct % 2]
                    project(ct, p_, r_p, 0, NT, ct)
                    if dbg is not None and dbg[0] == "p" and ct == dbg[2]:
                        dump(p_[:, 0:NT], r_p)
                    TS("dve", t1[:], p_[:], mu3[:, ct:ct + 1], None, ALU.mult, None, [r_p, r_mu3], [r_t1])
                    S.op("dve", lambda g: g.memset(so[:, 0:1], 0.0), [], [r_so])
                    STT(so[:, 1:NT], p_[:, 0:NT - 1], vec[:, mo + ct:mo + ct + 1], t1[:, 1:NT], ALU.mult, ALU.add, [r_p, r_t1, r_vec], [r_so])
                    STT(so[:, 0:NT - 1], p_[:, 1:NT], vec[:, mo + 26 + ct:mo + 27 + ct], so[:, 0:NT - 1], ALU.mult, ALU.add, [r_p, r_so, r_vec], [r_so])
                    DMA(pscr[ct], so[:], [r_so], [])
                    if dbg is not None and dbg[0] == "s" and ct == dbg[2]:
                        dump(so[:, 0:NT], r_so)
                for ct in range(26, 34):
                    p_, r_p = pt[ct % 2]
                    so, r_so = s_[ct % 2]
                    project(ct, p_, r_p, LAT0, LAT0 + SEG, ct)
                    ACT(so[:, LAT0:LAT0 + SEG], p_[:, LAT0:LAT0 + SEG], AF.Silu, [r_p], [r_so])
                    DMA(pscr[ct][:, LAT0:LAT0 + SEG], so[:, LAT0:LAT0 + SEG], [r_so], [])

                S.barrier()
                es2a.close()
                s_ = [SB(es2, "sp%d" % i, [128, SEG]) for i in range(1)]
                wpf, r_wpf = SB(es2, "wpf", [128, 4, 2, 256])
                wpb, r_wpb = SB(es2, "wpb", [128, 4, 2, 256], BF16)
                DMA(wpf[:], wpool, [], [r_wpf])
                CP("act", wpb[:].rearrange("p a b c -> p (a b c)"), wpf[:].rearrange("p a b c -> p (a b c)"), [r_wpf], [r_wpb])
                mpf, r_mpf = SB(es2, "mpf", [128, 4, 128])
                mpb, r_mpb = SB(es2, "mpb", [128, 4, 128], BF16)
                DMA(mpf[:], mpoolT, [], [r_mpf])
                CP("act", mpb[:].rearrange("p a b -> p (a b)"), mpf[:].rearrange("p a b -> p (a b)"), [r_mpf], [r_mpb])
                wgf = [SB(es2, "wgf%d" % i, [128, 8, 256]) for i in range(1)]
                wgb, r_wgb = SB(es2, "wgb", [128, 16, 256], BF16)
                utok, r_utok = SB(es2, "utok", [128, 256], BF16)
                zT, r_zT = SB(es2, "zT", [128, 2, SEG], BF16)
                mixo = [SB(es2, "mixo%d" % i, [128, SEG], BF16) for i in range(1)]
                for g4 in range(4):
                    wf_, r_wf = wgf[0]
                    c_in = 4352 + g4 * 256
                    for hf in range(2):
                        DMA(wf_[:], w_in[hf * 1024:(hf + 1) * 1024, c_in:c_in + 256].rearrange("(k p) c -> p k c", p=128), [], [r_wf])
                        CP("act", wgb[:, hf * 8:(hf + 1) * 8, :].rearrange("p a b -> p (a b)"), wf_[:].rearrange("p a b -> p (a b)"), [r_wf], [r_wgb])
                    for ti in range(16):
                        ps, r_ps = PS()
                        tc0 = LAT0 + ti * 128
                        for kc in range(16):
                            MM(ps[:, 0:256], hT[:, kc, tc0:tc0 + 128], wgb[:, kc, :], [r_hT, r_wgb], [r_ps], start=(kc == 0), stop=(kc == 15))
                        CP("act", utok[:], ps[:, 0:256], [r_ps], [r_utok])
                        ps2, r_ps2 = PS()
                        for cc in range(2):
                            MM(ps2[:, cc * 128:(cc + 1) * 128], utok[:, cc * 128:(cc + 1) * 128], mpb[:, g4, :], [r_utok, r_mpb], [r_ps2])
                        for cc in range(2):
                            CP("dve", zT[:, cc, ti * 128:(ti + 1) * 128], ps2[:, cc * 128:(cc + 1) * 128], [r_ps2], [r_zT])
                    for dc in range(2):
                        q = g4 * 2 + dc
                        p_, r_p = pt[q % 2]
                        so, r_so = s_[0]
                        project(42 + q, p_, r_p, LAT0, LAT0 + SEG, q)
                        ACT(so[:, 0:SEG], p_[:, LAT0:LAT0 + SEG], AF.Silu, [r_p], [r_so])
                        mo_, r_mo = mixo[0]
                        for tq in range(4):
                            ps, r_ps = PS()
                            for cc in range(2):
                                MM(ps[:, :], wpb[:, g4, cc, dc * 128:(dc + 1) * 128], zT[:, cc, tq * 512:(tq + 1) * 512], [r_wpb, r_zT], [r_ps], start=(cc == 0), stop=(cc == 1))
                            STT(mo_[:, tq * 512:(tq + 1) * 512], ps[:, :], V("pool_scale", q), so[:, tq * 512:(tq + 1) * 512], ALU.mult, ALU.mult, [r_ps, r_vec, r_so], [r_mo])
                        DMA(mixscr[8 + q], mo_[:], [r_mo], [])
                S.barrier()

        if dbg is not None and dbg[0] == "poolend":
            dump(gtbc[:, 0:256], r_gtbc)
        with ExitStack() as esC:
            for i in range(S.NDMA):
                if S.dcnt[i] > 0:
                    S._wait("sp", i, S.dcnt[i])
            Z = [SB(esC, "Z%d" % i, [128, NT]) for i in range(3)]
            maskb = []
            DMA(Z[0][0][:, 0:1280], consts[:, 320:1600], [], [Z[0][1]])
            for d in range(2):
                m, r_m = SB(esC, "mask%d" % d, [128, 640], BF16)
                CP("dve", m[:], Z[0][0][:, 640 * d:640 * (d + 1)], [Z[0][1]], [r_m])
                maskb.append((m, r_m))
            DMA(Z[1][0][:, 0:1792], consts[:, 1600:3392], [], [Z[1][1]])
            mlev = []
            for j in range(2):
                m, r_m = SB(esC, "mlev%d" % j, [128, 7, 128], BF16)
                CP("dve", m[:].rearrange("p a b -> p (a b)"), Z[1][0][:, 896 * j:896 * (j + 1)], [Z[1][1]], [r_m])
                mlev.append((m, r_m))
            S.barrier()
            lora = []
            for zi, (nm, cti) in enumerate((("tdw", 24), ("da", 25))):
                lf, r_lf = Z[zi]
                lb, r_lb = SB(esC, nm + "b", [128, NT], BF16)
                DMA(lf[:], pscr[cti], [], [r_lf])
                if nm == "tdw":
                    ACT(lb[:], lf[:], AF.Tanh, [r_lf], [r_lb])
                else:
                    CP("dve", lb[:], lf[:], [r_lf], [r_lb])
                lora.append((lb, r_lb))
            (tdw, r_tdw), (dab, r_dab) = lora
            upf, r_upf = Z[2]
            upb, r_upb = SB(esC, "upb", [128, 2, 1024], BF16)
            DMA(upf[:, 0:1024], wup, [], [r_upf])
            DMA(upf[:, 1024:2048], aup, [], [r_upf])
            CP("act", upb[:].rearrange("p a b -> p (a b)"), upf[:, 0:2048], [r_upf], [r_upb])

            r32, r_r = SB(esC, "r32", [128, NT])
            k32, r_k = SB(esC, "k32", [128, NT])
            v32, r_v = SB(esC, "v32", [128, NT])
            kk32, r_kk = SB(esC, "kk32", [128, NT])
            ksum, r_ks = SB(esC, "ksum", [128, NT])
            opA, r_opA = SB(esC, "opA", [128, NT], BF16)
            opR, r_opR = SB(esC, "opR", [128, NT], BF16)
            opB, r_opB = SB(esC, "opB", [128, NT], BF16)
            opK, r_opK = SB(esC, "opK", [128, NT], BF16)
            vb, r_vb = opA, r_opA
            Vtok, r_Vtok = SB(esC, "Vtok", [128, NCH, 2, 128], BF16)
            S.op("dve", lambda g: g.memset(Vtok[:].rearrange("p a b c -> p (a b c)"), 0.0), [], [r_Vtok])
            ABK = [SB(esC, "ABK%d" % i, [128, 6, 128], BF16) for i in range(2)]
            for (t_, r_t) in ABK:
                S.op("dve", lambda g: g.memset(t_[:].rearrange("p a b -> p (a b)"), 0.0), [], [r_t])
            XP = [[SB(esC, "XP%d_%d" % (i, h), [128, 2, 128], BF16) for h in range(2)] for i in range(2)]
            for i in range(2):
                for h in range(2):
                    t_, r_t = XP[i][h]
                    S.op("dve", lambda g: g.memset(t_[:].rearrange("p a b -> p (a b)"), 0.0), [], [r_t])
            SBA = [SB(esC, "SBA%d" % i, [128, 640], BF16) for i in range(4)]
            TB = [SB(esC, "TB%d" % i, [128, 128], BF16) for i in range(16)]
            X0s = [[SB(esC, "X0_%d_%d" % (i, h), [128, 128], BF16) for h in range(2)] for i in range(2)]
            EA = [[SB(esC, "EA%d_%d" % (h, j), [128, 7, 128], BF16) for j in range(2)] for h in range(2)]
            GTs = [[SB(esC, "GT%d_%d" % (d, c), [128, 128], BF16) for c in range(NCHL)] for d in range(2)]
            PhT = [[SB(esC, "PhT%d_%d" % (d, c), [128, 128], BF16) for c in range(NCH)] for d in range(2)]
            for d in range(2):
                for c in range(NCH):
                    t_, r_t = PhT[d][c]
                    S.op("dve", lambda g: g.memset(t_[:, :], 0.0), [], [r_t])
            Qg = [[SB(esC, "Qg%d_%d" % (d, c), [128, 64]) for c in range(NCH)] for d in range(2)]
            GC = [SB(esC, "GC%d" % d, [128, NCH]) for d in range(2)]
            NB, r_NB = SB(esC, "NB", [128, NCH])
            PB, r_PB = SB(esC, "PB", [128, NCH])
            Yacc, r_Y = SB(esC, "Yacc", [128, SEG])
            PQa = [SB(esC, "PQa%d" % i, [128, 128], BF16) for i in range(2)]
            PQf, r_PQf = SB(esC, "PQf", [128, 128])
            PQfb, r_PQfb = SB(esC, "PQfb", [128, 128], BF16)
            S.op("dve", lambda g: g.memset(PQfb[:, :], 0.0), [], [r_PQfb])
            BDs = [SB(esC, "BDs%d" % i, [128, 128], BF16) for i in range(2)]
            HBD = [SB(esC, "HBD%d" % i, [128, 128], BF16) for i in range(2)]
            for (t_, r_t) in BDs + HBD:
                S.op("dve", lambda g: g.memset(t_[:, :], 0.0), [], [r_t])
            AGi, r_AGi = SB(esC, "AGi", [128, 2, 128])
            AGo, r_AGo = SB(esC, "AGo", [128, 8, 256])
            AGb, r_AGb = SB(esC, "AGb", [128, 8, 256], BF16)
            Hctx = [SB(esC, "Hctx%d" % d, [128, 64]) for d in range(2)]
            Hf, r_Hf = SB(esC, "Hf", [128, 64])
            Hb = [SB(esC, "Hb%d" % i, [128, 64], BF16) for i in range(2)]
            T1, r_T1 = SB(esC, "T1", [128, 64])
            mixo, r_mixo = opR, r_opR
            xcnt = [0]

            for hp in range(8):
                DMA(r32[:], pscr[hp], [], [r_r])
                DMA(k32[:], pscr[8 + hp], [], [r_k])
                DMA(v32[:], pscr[16 + hp], [], [r_v])
                CP("act", vb[:], v32[:], [r_v], [r_vb])
                (Z1, r_Z1), (Z2, r_Z2), (Z3, r_Z3) = Z
                TS("dve", Z1[:], k32[:], V("k_k", hp), None, ALU.mult, None, [r_k, r_vec], [r_Z1])
                ACT(Z2[:], Z1[:], AF.Square, [r_Z1], [r_Z2])
                for (t0, tn) in TOKT:
                    ps, r_ps = PS()
                    MM(ps[:, 0:tn], blk1, Z2[:, t0:t0 + tn], [r_cst, r_Z2], [r_ps])
                    TS("dve", Z3[:, t0:t0 + tn], ps[:, 0:tn], 1e-24, None, ALU.max, None, [r_ps], [r_Z3])
                ACT(Z3[:], Z3[:], AF.Sqrt, [r_Z3], [r_Z3])
                S.op("dve", lambda g: g.reciprocal(out=Z3[:], in_=Z3[:]), [r_Z3], [r_Z3])
                TT("dve", kk32[:], Z1[:], Z3[:], ALU.mult, [r_Z1, r_Z3], [r_kk])
                if dbg is not None and dbg[0] == "c1" and hp == dbg[2]:
                    dump(kk32[:, 0:256], r_kk)
                for c in range(NCH):
                    c0 = cstart(c)
                    TR(pst[:, 0:128], vb[:, c0:c0 + C], identb[:], [r_vb, r_idb], [r_pst])
                    CP("act", Vtok[:, c, 0, 0:64], pst[:, 0:64], [r_pst], [r_Vtok])
                    CP("dve", Vtok[:, c, 1, 64:128], pst[:, 64:128], [r_pst], [r_Vtok])

                for d in range(2):
                    mk, r_mk = maskb[d]
                    GC_, r_GC = GC[d]
                    for (t0, tn) in TOKT:
                        ps, r_ps = PS()
                        MM(ps[:, 0:tn], upb[64 * d:64 * d + 64, 0, hp * 128:(hp + 1) * 128], tdw[64 * d:64 * d + 64, t0:t0 + tn], [r_upb, r_tdw], [r_ps])
                        ACT(Z1[:, t0:t0 + tn], ps[:, 0:tn], AF.Sigmoid, [r_ps, r_vec], [r_Z1], bias=V("w0", hp * 2 + d))
                    TS("dve", Z1[:], Z1[:], -CEXP, None, ALU.mult, None, [r_Z1], [r_Z1])
                    S.op("dve", lambda g: g.memset(Z3[:], 1.0), [], [r_Z3])
                    S.op("dve", lambda g: g.tensor_tensor_scan(out=Z2[:], data0=Z3[:], data1=Z1[:], initial=0.0, op0=ALU.mult, op1=ALU.add), [r_Z3, r_Z1], [r_Z2])
                    TT("dve", Z1[:], Z2[:], Z1[:], ALU.subtract, [r_Z2, r_Z1], [r_Z1])
                    for c in range(NCH):
                        c0 = cstart(c)
                        bc = c0 - 1 if d == 0 else c0 + C - 1
                        CP("dve", PB[:, c:c + 1], Z2[:, bc:bc + 1], [r_Z2], [r_PB])
                    TS("dve", NB[:], PB[:], -1.0, None, ALU.mult, None, [r_PB], [r_NB])
                    for c in range(NCH):
                        c0 = cstart(c)
                        sl = slice(c0, c0 + C)
                        if d == 0:
                            ACT(Z3[:, sl], Z2[:, sl], AF.Exp, [r_Z2, r_NB], [r_Z3], bias=NB[:, c:c + 1])
                            ACT(Z1[:, sl], Z1[:, sl], AF.Exp, [r_Z1, r_NB], [r_Z1], bias=NB[:, c:c + 1])
                        else:
                            ACT(Z3[:, sl], Z1[:, sl], AF.Exp, [r_Z1, r_PB], [r_Z3], bias=PB[:, c:c + 1], scale=-1.0)
                    for c in range(NCH):
                        c0 = cstart(c)
                        ge = c0 + C - 1 if d == 0 else c0
                        CP("dve", GC_[:, c:c + 1], Z3[:, ge:ge + 1], [r_Z3], [r_GC])
                    TT("dve", opR[:], r32[:], Z3[:], ALU.mult, [r_r, r_Z3], [r_opR])
                    if d == 0:
                        Eexc, r_Ee = Z1, r_Z1
                    else:
                        for c in range(NCH):
                            c0 = cstart(c)
                            sl = slice(c0, c0 + C)
                            ACT(Z2[:, sl], Z2[:, sl], AF.Exp, [r_Z2, r_PB], [r_Z2], bias=PB[:, c:c + 1], scale=-1.0)
                        Eexc, r_Ee = Z2, r_Z2
                    STT(opA[:], kk32[:], -1.0, Eexc[:], ALU.mult, ALU.mult, [r_kk, r_Ee], [r_opA])
                    for c in range(NCH):
                        c0 = cstart(c)
                        sl = slice(c0, c0 + C)
                        if d == 0:
                            ACT(Z3[:, sl], Z2[:, sl], AF.Exp, [r_Z2, r_PB], [r_Z3], bias=PB[:, c:c + 1], scale=-1.0)
                        else:
                            ACT(Z3[:, sl], Z1[:, sl], AF.Exp, [r_Z1, r_NB], [r_Z3], bias=NB[:, c:c + 1])
                    for (t0, tn) in TOKT:
                        ps, r_ps = PS()
                        MM(ps[:, 0:tn], upb[64 * d:64 * d + 64, 1, hp * 128:(hp + 1) * 128], dab[64 * d:64 * d + 64, t0:t0 + tn], [r_upb, r_dab], [r_ps])
                        ACT(Z1[:, t0:t0 + tn], ps[:, 0:tn], AF.Sigmoid, [r_ps, r_vec], [r_Z1], bias=V("a0", hp * 2 + d))
                    TT("dve", Z2[:], kk32[:], Z1[:], ALU.mult, [r_kk, r_Z1], [r_Z2])
                    TT("dve", opB[:], Z2[:], Z3[:], ALU.mult, [r_Z2, r_Z3], [r_opB])
                    TS("dve", Z2[:, 0:1], V("k_a", hp), -1.0, 1.0, ALU.mult, ALU.add, [r_vec], [r_Z2])
                    TS("dve", Z1[:], Z1[:], V("k_a", hp), None, ALU.mult, None, [r_Z1, r_vec], [r_Z1])
                    TS("dve", Z1[:], Z1[:], Z2[:, 0:1], None, ALU.add, None, [r_Z1, r_Z2], [r_Z1])
                    TT("dve", Z1[:], Z1[:], k32[:], ALU.mult, [r_Z1, r_k], [r_Z1])
                    TT("dve", opK[:], Z1[:], Z3[:], ALU.mult, [r_Z1, r_Z3], [r_opK])
                    if d == 0:
                        CP("act", ksum[:], Z1[:], [r_Z1], [r_ks])
                    else:
                        TT("dve", ksum[:], ksum[:], Z1[:], ALU.add, [r_ks, r_Z1], [r_ks])

                    if dbg is not None and dbg[0] == "c2":
                        dump(Z1[:, 0:256], r_Z1)
                    for c in range(NCH):
                        c0 = cstart(c)
                        sl = slice(c0, c0 + C)
                        lat = c < NCHL
                        abk, r_abk = ABK[c % 2]
                        abf = abk[:].rearrange("p a b -> p (a b)")
                        for j, (op_, r_op) in enumerate(((opA, r_opA), (opB, r_opB), (opK, r_opK))):
                            TR(pst[:, 128 * (j + 1):128 * (j + 2)], op_[:, sl], identb[:], [r_op, r_idb], [r_pst])
                        CP("act", abk[:, 0, :], pst[:, 128:256], [r_pst], [r_abk])
                        CP("dve", abf[:, 128:512].rearrange("p (h x) -> p h x", x=192)[:, :, 0:64],
                           pst[:, 256:384].rearrange("p (h x) -> p h x", x=64), [r_pst], [r_abk])
                        CP("act", abf[:, 384:768].rearrange("p (h x) -> p h x", x=192)[:, :, 0:64],
                           pst[:, 384:512].rearrange("p (h x) -> p h x", x=64), [r_pst], [r_abk])
                        heads = []
                        for h2 in range(2):
                            hb = 64 * h2
                            hs = slice(hb, hb + 64)
                            sba, r_sba = SBA[(2 * c + h2) % 4]
                            psA, r_psA = PS()
                            MM(psA[:, 0:128], opA[hs, sl], opB[hs, sl], [r_opA, r_opB], [r_psA])
                            MM(psA[:, 128:256], opB[hs, sl], opA[hs, sl], [r_opA, r_opB], [r_psA])
                            MM(psA[:, 256:384], opB[hs, sl], opR[hs, sl], [r_opR, r_opB], [r_psA])
                            MM(psA[:, 384:512], opK[hs, sl], opA[hs, sl], [r_opA, r_opK], [r_psA])
                            TT("dve", sba[:, 0:512], psA[:, :], mk[:, 0:512], ALU.mult, [r_psA, r_mk], [r_sba])
                            psB, r_psB = PS()
                            if lat:
                                MM(psB[:, 0:128], opK[hs, sl], opR[hs, sl], [r_opK, r_opR], [r_psB])
                            MM(psB[:, 128:192], sba[:, 384:512], Vtok[:, c, h2, hs], [r_sba, r_Vtok], [r_psB])
                            if lat:
                                TT("dve", sba[:, 512:640], psB[:, 0:128], mk[:, 512:640], ALU.mult, [r_psB, r_mk], [r_sba])
                            X, r_X = X0s[c % 2][h2]
                            CP("dve", X[:, 0:64], abk[:, 0, hs], [r_abk], [r_X])
                            CP("act", X[:, 64:128], psB[:, 128:192], [r_psB], [r_X])
                            heads.append(dict(h2=h2, hb=hb, hs=hs, sba=sba, r_sba=r_sba, X=X, r_X=r_X, N=sba[:, 0:128], NTr=sba[:, 128:256], r_N=r_sba))
                        if dbg is not None and dbg[0] == "c3":
                            dump(Z3[:, 0:256], r_Z3)
                        if dbg is not None and dbg[0] == "m1":
                            CP("dve", Z3[:, 0:640], heads[0]["sba"][:, 0:640], [heads[0]["r_sba"]], [r_Z3])
                            dump(Z3[:, 0:640], r_Z3)
                        if dbg is not None and dbg[0] == "m0":
                            CP("dve", Z3[:, 0:128], heads[0]["X"][:, :], [heads[0]["r_X"]], [r_Z3])
                            dump(Z3[:, 0:128], r_Z3)
                        def tb():
                            t = TB[xcnt[0] % 16]
                            xcnt[0] += 1
                            return t
                        mE, r_mE = mlev[d]
                        mET, r_mET = mlev[1 - d]
                        for H in heads:
                            ea, r_ea = EA[H["h2"]][0]
                            eta, r_eta = EA[H["h2"]][1]
                            TT("dve", ea[:], mE[:], H["N"].unsqueeze(1).broadcast_to([128, 7, 128]), ALU.mult, [r_mE, H["r_N"]], [r_ea])
                            TT("dve", eta[:], mET[:], H["NTr"].unsqueeze(1).broadcast_to([128, 7, 128]), ALU.mult, [r_mET, H["r_N"]], [r_eta])
                            H["T"], H["r_T"] = identb, r_idb
                            H["TT"], H["r_TT"] = identb, r_idb
                        for lv in range(7):
                            for H in heads:
                                ea, r_ea = EA[H["h2"]][0]
                                eta, r_eta = EA[H["h2"]][1]
                                ps1, r_ps1 = PS()
                                MM(ps1[:, 0:128], ea[:, lv, :], H["TT"][:, :], [r_ea, H["r_TT"]], [r_ps1])
                                gp, r_gp = tb()
                                CP("act", gp[:, :], ps1[:, 0:128], [r_ps1], [r_gp])
                                if lv < 6:
                                    ps3, r_ps3 = PS()
                                    MM(ps3[:, 0:128], eta[:, lv, :], H["T"][:, :], [r_eta, H["r_T"]], [r_ps3])
                                    gq, r_gq = tb()
                                    CP("act", gq[:, :], ps3[:, 0:128], [r_ps3], [r_gq])
                                ps2, r_ps2 = PS()
                                MM(ps2[:, 0:128], H["T"][:, :], gp[:, :], [H["r_T"], r_gp], [r_ps2])
                                ttn, r_ttn = tb()
                                TT("dve", ttn[:, :], ps2[:, 0:128], H["TT"][:, :], ALU.add, [r_ps2, H["r_TT"]], [r_ttn])
                                if lv < 6:
                                    ps4, r_ps4 = PS()
                                    MM(ps4[:, 0:128], H["TT"][:, :], gq[:, :], [H["r_TT"], r_gq], [r_ps4])
                                    tn, r_tn = tb()
                                    TT("dve", tn[:, :], ps4[:, 0:128], H["T"][:, :], ALU.add, [r_ps4, H["r_T"]], [r_tn])
                                    H["T"], H["r_T"] = tn, r_tn
                                H["TT"], H["r_TT"] = ttn, r_ttn
                        for H in heads:
                            psx, r_psx = PS()
                            MM(psx[:, 0:128], H["TT"][:, :], H["X"][:, :], [H["r_TT"], H["r_X"]], [r_psx])
                            xp, r_xp = XP[c % 2][H["h2"]]
                            CP("dve", xp[:, :, H["hs"]], psx[:, 0:128].rearrange("p (a b) -> p a b", b=64), [r_psx], [r_xp])
                            H["xp"], H["r_xp"] = xp, r_xp
                        if dbg is not None and dbg[0] == "c4":
                            dump(Z3[:, 0:256], r_Z3)
                        if dbg is not None and dbg[0] == "m2":
                            CP("dve", Z3[:, 0:256], heads[0]["xp"][:].rearrange("p a b -> p (a b)"), [heads[0]["r_xp"]], [r_Z3])
                            dump(Z3[:, 0:256], r_Z3)
                        psP, r_psP = PS()
                        psG, r_psG = PS()
                        for H in heads:
                            h2, hs, xp, r_xp = H["h2"], H["hs"], H["xp"], H["r_xp"]
                            MM(psP[:, 0:64], xp[:, 0, :], abk[:, 1 + h2, hs], [r_xp, r_abk], [r_psP], start=(h2 == 0), stop=(h2 == 1))
                        for H in heads:
                            h2, hs, xp, r_xp = H["h2"], H["hs"], H["xp"], H["r_xp"]
                            MM(psP[:, 64:128], abk[:, 1 + h2, :], xp[:, 1, hs], [r_xp, r_abk], [r_psP], start=(h2 == 0), stop=False)
                            MM(psP[:, 64:128], abk[:, 3 + h2, :], Vtok[:, c, h2, hs], [r_abk, r_Vtok], [r_psP], start=False, stop=(h2 == 1))
                        if lat:
                            for H in heads:
                                h2, hs, xp, r_xp = H["h2"], H["hs"], H["xp"], H["r_xp"]
                                MM(psG[:, 0:128], xp[:, 0, :], H["sba"][:, 256:384], [r_xp, H["r_sba"]], [r_psG], start=(h2 == 0), stop=(h2 == 1))
                            for H in heads:
                                h2, hs, xp, r_xp = H["h2"], H["hs"], H["xp"], H["r_xp"]
                                MM(psG[:, 128:256], xp[:, 1, :], H["sba"][:, 256:384], [r_xp, H["r_sba"]], [r_psG], start=(h2 == 0), stop=False)
                                MM(psG[:, 128:256], Vtok[:, c, h2, :], H["sba"][:, 512:640], [r_Vtok, H["r_sba"]], [r_psG], start=False, stop=(h2 == 1))
                        ph, r_ph = PhT[d][c]
                        qg, r_qg = Qg[d][c]
                        for h2 in range(2):
                            hs = slice(64 * h2, 64 * h2 + 64)
                            TT("dve", ph[hs, hs], psP[hs, 0:64], cst[hs, 256:320], ALU.add, [r_psP, r_cst], [r_ph])
                        TS("dve", qg[:, :], psP[:, 64:128], GC_[:, c:c + 1], None, ALU.mult, None, [r_psP, r_GC], [r_qg])
                        if lat:
                            gt_, r_gt = GTs[d][c]
                            TT("dve", gt_[:, :], psG[:, 0:128], opR[:, sl], ALU.add, [r_psG, r_opR], [r_gt])
                            ys = slice(c * C, (c + 1) * C)
                            if d == 0:
                                CP("act", Yacc[:, ys], psG[:, 128:256], [r_psG], [r_Y])
                            else:
                                TT("dve", Yacc[:, ys], psG[:, 128:256], Yacc[:, ys], ALU.add, [r_psG, r_Y], [r_Y])

                if dbg is not None and dbg[0] == "c5" and hp == dbg[2]:
                    dump(Qg[1][3][0][:, 0:64], Qg[1][3][1])
                for d in range(2):
                    GC_, r_GC = GC[d]
                    order = list(range(NCHL)) if d == 0 else list(range(NCHL - 1, -1, -1))
                    pq, r_pq = PQa[0]
                    CP("dve", pq[:, 0:64], idstk, [r_cst], [r_pq])
                    S.op("dve", lambda g: g.memset(pq[:, 64:128], 0.0), [], [r_pq])
                    for i, c in enumerate(order):
                        ph, r_ph = PhT[d][c]
                        qg, r_qg = Qg[d][c]
                        ps, r_ps = PS()
                        MM(ps[:, 0:128], ph[:, :], pq[:, :], [r_ph, r_pq], [r_ps])
                        if i < NCHL - 1:
                            pqn, r_pqn = PQa[(i + 1) % 2]
                            TS("dve", pqn[:, 0:64], ps[:, 0:64], GC_[:, c:c + 1], None, ALU.mult, None, [r_ps, r_GC], [r_pqn])
                            STT(pqn[:, 64:128], ps[:, 64:128], GC_[:, c:c + 1], qg[:, :], ALU.mult, ALU.add, [r_ps, r_GC, r_qg], [r_pqn])
                            pq, r_pq = pqn, r_pqn
                        else:
                            TS("dve", PQf[:, 0:64], ps[:, 0:64], GC_[:, c:c + 1], None, ALU.mult, None, [r_ps, r_GC], [r_PQf])
                            STT(PQf[:, 64:128], ps[:, 64:128], GC_[:, c:c + 1], qg[:, :], ALU.mult, ALU.add, [r_ps, r_GC, r_qg], [r_PQf])
                    if dbg is not None and dbg[0] == "s1" and hp == dbg[2]:
                        dump(PQf[:, :], r_PQf)
                    for h2 in range(2):
                        hs = slice(64 * h2, 64 * h2 + 64)
                        CP("dve", PQfb[hs, hs], PQf[hs, 0:64], [r_PQf], [r_PQfb])
                    TR(pst[:, 0:128], PQfb[:, :], identb[:], [r_PQfb, r_idb], [r_pst])
                    for h2 in range(2):
                        hs = slice(64 * h2, 64 * h2 + 64)
                        CP("dve", AGi[hs, d, 0:64], pst[hs, hs], [r_pst], [r_AGi])
                    CP("act", AGi[:, d, 64:128], PQf[:, 64:128], [r_PQf], [r_AGi])
                    if dbg is not None and dbg[0] == "s2" and hp == dbg[2]:
                        dump(AGi[:].rearrange("p a b -> p (a b)"), r_AGi)
                    cord = [16, 17] if d == 0 else [17, 16]
                    hb0, r_hb0 = Hb[0]
                    CP("dve", hb0[:, :], Qg[d][cord[0]][0][:, :], [Qg[d][cord[0]][1]], [r_hb0])
                    ps, r_ps = PS()
                    ph, r_ph = PhT[d][cord[1]]
                    MM(ps[:, 0:64], ph[:, :], hb0[:, :], [r_ph, r_hb0], [r_ps])
                    hc, r_hc = Hctx[d]
                    STT(hc[:, :], ps[:, 0:64], GC_[:, cord[1]:cord[1] + 1], Qg[d][cord[1]][0][:, :], ALU.mult, ALU.add, [r_ps, r_GC, Qg[d][cord[1]][1]], [r_hc])

                if dbg is not None and dbg[0] == "preag" and hp == dbg[2]:
                    dump(AGi[:].rearrange("p a b -> p (a b)"), r_AGi)
                t_st = DMA(agin[hp].ap(), AGi[:].rearrange("p a b -> p (a b)"), [r_AGi], [])
                S._wait("pool", *t_st)
                if not S.dead:
                    cc = nc.gpsimd.collective_compute("AllGather", ALU.bypass, replica_groups=[list(range(8))],
                                                      ins=[agin[hp].ap().opt()], outs=[agout[hp].ap().opt()])
                    cc_cnt[0] += 1
                    cc.then_inc(cc_sem, 1)
                    nc.sync.wait_ge(cc_sem, cc_cnt[0])
                DMA(AGo[:], agout[hp].ap().rearrange("(r p) c -> p r c", p=128), [], [r_AGo])
                CP("dve", AGb[:].rearrange("p a b -> p (a b)"), AGo[:].rearrange("p a b -> p (a b)"), [r_AGo], [r_AGb])
                if dbg is not None and dbg[0] == "ag" and hp == dbg[2]:
                    dump(AGo[:].rearrange("p a b -> p (a b)"), r_AGo)

                for d in range(2):
                    GC_, r_GC = GC[d]
                    hc, r_hc = Hctx[d]
                    CP("dve", Hf[:, :], hc[:, :], [r_hc], [r_Hf])
                    hcur, r_hcur = Hb[0]
                    CP("act", hcur[:, :], hc[:, :], [r_hc], [r_hcur])
                    ranks = list(range(8)) if d == 0 else list(range(7, -1, -1))
                    cmo = VEC["cmask"][0]
                    for ri, r_ in enumerate(ranks):
                        bd, r_bd = BDs[ri % 2]
                        TT("dve", bd[:, :].rearrange("p (a b) -> p a b", b=64), blk1.rearrange("p (a b) -> p a b", b=64),
                           AGb[:, r_, d * 128:d * 128 + 64].unsqueeze(1).broadcast_to([128, 2, 64]), ALU.mult, [r_AGb, r_cst], [r_bd])
                        ps, r_ps = PS()
                        MM(ps[:, 0:64], bd[:, :], hcur[:, :], [r_bd, r_hcur], [r_ps])
                        TT("dve", T1[:, :], ps[:, 0:64], AGo[:, r_, d * 128 + 64:d * 128 + 128], ALU.add, [r_ps, r_AGo], [r_T1])
                        TT("dve", T1[:, :], T1[:, :], Hf[:, :], ALU.subtract, [r_T1, r_Hf], [r_T1])
                        STT(Hf[:, :], T1[:, :], vec[:, cmo + d * 8 + r_: cmo + d * 8 + r_ + 1], Hf[:, :], ALU.mult, ALU.add, [r_T1, r_vec, r_Hf], [r_Hf])
                        CP("act", hcur[:, :], Hf[:, :], [r_Hf], [r_hcur])
                    if dbg is not None and dbg[0] == "st":
                        dump(Hf[:, :], r_Hf)
                    hbd, r_hbd = HBD[0]
                    TT("dve", hbd[:, :].rearrange("p (a b) -> p a b", b=64), blk1.rearrange("p (a b) -> p a b", b=64),
                       Hf[:, :].unsqueeze(1).broadcast_to([128, 2, 64]), ALU.mult, [r_Hf, r_cst], [r_hbd])
                    order = list(range(NCHL)) if d == 0 else list(range(NCHL - 1, -1, -1))
                    hi = 0
                    if dbg is not None and dbg[0] == "p2c0":
                        dump(Yacc[:, 0:64], r_Y)
                    if dbg is not None and dbg[0] == "p2c0h":
                        dump(Hf[:, :], r_Hf)
                    for c in order:
                        gt_, r_gt = GTs[d][c]
                        ph, r_ph = PhT[d][c]
                        qg, r_qg = Qg[d][c]
                        psy, r_psy = PS()
                        ps, r_ps = PS()
                        MM(psy[:, 0:128], hbd[:, :], gt_[:, :], [r_hbd, r_gt], [r_psy])
                        if dbg is not None and dbg[0] == "p2c1":
                            dump(Yacc[:, 0:64], r_Y)
                        MM(ps[:, 0:64], ph[:, :], hcur[:, :], [r_ph, r_hcur], [r_ps])
                        ys = slice(c * C, (c + 1) * C)
                        if dbg is not None and dbg[0] == "p2b0":
                            dump(Yacc[:, 0:64], r_Y)
                        TT("dve", Yacc[:, ys], psy[:, 0:128], Yacc[:, ys], ALU.add, [r_psy, r_Y], [r_Y])
                        if dbg is not None and dbg[0] == "p2b1":
                            dump(Yacc[:, 0:64], r_Y)
                        hi += 1
                        hn, r_hn = Hb[hi % 2]
                        hbn, r_hbn = HBD[hi % 2]
                        STT(hn[:, :], ps[:, 0:64], GC_[:, c:c + 1], qg[:, :], ALU.mult, ALU.add, [r_ps, r_GC, r_qg], [r_hn])
                        TT("dve", hbn[:, :].rearrange("p (a b) -> p a b", b=64), blk1.rearrange("p (a b) -> p a b", b=64),
                           hn[:, :].unsqueeze(1).broadcast_to([128, 2, 64]), ALU.mult, [r_hn, r_cst], [r_hbn])
                        hcur, r_hcur = hn, r_hn
                        hbd, r_hbd = hbn, r_hbn
                        if dbg is not None and dbg[0] == "p2a" and hi == dbg[2]:
                            dump(Yacc[:, 0:64], r_Y)
                if dbg is not None and dbg[0] == "y" and hp == dbg[2]:
                    dump(Yacc[:, :], r_Y)

                (Z1, r_Z1), (Z2, r_Z2), (Z3, r_Z3) = Z
                LS = slice(LAT0, LAT0 + SEG)
                ACT(Z1[:, 0:SEG], Yacc[:, :], AF.Square, [r_Y], [r_Z1])
                STT(kk32[:, 0:SEG], r32[:, LS], V("r_k", hp), ksum[:, LS], ALU.mult, ALU.mult, [r_r, r_vec, r_ks], [r_kk])
                for tq in range(4):
                    ts_ = slice(tq * 512, (tq + 1) * 512)
                    ps1, r_ps1 = PS()
                    ps2, r_ps2 = PS()
                    ps3, r_ps3 = PS()
                    MM(ps1[:, :], blk1, Yacc[:, ts_], [r_cst, r_Y], [r_ps1])
                    MM(ps2[:, :], blk1, Z1[:, ts_], [r_cst, r_Z1], [r_ps2])
                    MM(ps3[:, :], blk1, kk32[:, ts_], [r_cst, r_kk], [r_ps3])
                    TS("dve", Z2[:, ts_], ps1[:, :], 1.0 / 64, None, ALU.mult, None, [r_ps1], [r_Z2])
                    TT("dve", Z3[:, ts_], Z2[:, ts_], Z2[:, ts_], ALU.mult, [r_Z2], [r_Z3])
                    STT(Z3[:, ts_], ps2[:, :], 1.0 / 64, Z3[:, ts_], ALU.mult, ALU.subtract, [r_ps2, r_Z3], [r_Z3])
                    TS("dve", ksum[:, ts_], ps3[:, :], 1.0, None, ALU.mult, None, [r_ps3], [r_ks])
                if dbg is not None and dbg[0] == "g1":
                    dump(Z3[:, 0:SEG], r_Z3)
                ACT(Z3[:, 0:SEG], Z3[:, 0:SEG], AF.Sqrt, [r_Z3], [r_Z3], bias=64e-5)
                S.op("dve", lambda g: g.reciprocal(out=Z3[:, 0:SEG], in_=Z3[:, 0:SEG]), [r_Z3], [r_Z3])
                TT("dve", Z2[:, 0:SEG], Yacc[:, :], Z2[:, 0:SEG], ALU.subtract, [r_Y, r_Z2], [r_Z2])
                TT("dve", Z2[:, 0:SEG], Z2[:, 0:SEG], Z3[:, 0:SEG], ALU.mult, [r_Z2, r_Z3], [r_Z2])
                TS("dve", Z2[:, 0:SEG], Z2[:, 0:SEG], V("lnx_g", hp), V("lnx_b", hp), ALU.mult, ALU.add, [r_Z2, r_vec], [r_Z2])
                TT("dve", Z3[:, 0:SEG], ksum[:, 0:SEG], v32[:, LS], ALU.mult, [r_ks, r_v], [r_Z3])
                TT("dve", Z2[:, 0:SEG], Z2[:, 0:SEG], Z3[:, 0:SEG], ALU.add, [r_Z2, r_Z3], [r_Z2])
                DMA(Z1[:, 0:SEG], pscr[26 + hp][:, LS], [], [r_Z1])
                if dbg is not None and dbg[0] == "g4":
                    dump(Z1[:, 0:SEG], r_Z1)
                TT("dve", mixo[:, 0:SEG], Z2[:, 0:SEG], Z1[:, 0:SEG], ALU.mult, [r_Z2, r_Z1], [r_mixo])
                DMA(mixscr[hp], mixo[:, 0:SEG], [r_mixo], [])
                if dbg is not None and dbg[0] == "mix" and hp == dbg[2]:
                    dump(Z2[:, 0:SEG], r_Z2)
            S.barrier()

        with ExitStack() as esD:
            for i in range(S.NDMA):
                if S.dcnt[i] > 0:
                    S._wait("sp", i, S.dcnt[i])
            wob, r_wob = SB(esD, "wob", [128, 16, D], BF16)
            wof = [SB(esD, "wof%d" % i, [128, 2, D]) for i in range(2)]
            for k2 in range(8):
                wf_, r_wf = wof[k2 % 2]
                DMA(wf_[:], w_out[k2 * 256:(k2 + 1) * 256, :].rearrange("(k p) c -> p k c", p=128), [], [r_wf])
                CP("act" if k2 % 2 else "dve", wob[:, 2 * k2:2 * k2 + 2, :].rearrange("p a b -> p (a b)"), wf_[:].rearrange("p a b -> p (a b)"), [r_wf], [r_wob])
            fg, r_fg = SB(esD, "fg", [128, D])
            DMA(fg[:], fgb, [], [r_fg])
            mixt = [SB(esD, "mixt%d" % i, [128, 16, 128], BF16) for i in range(2)]
            xr = [SB(esD, "xr%d" % i, [128, D]) for i in range(2)]
            xn_ = [SB(esD, "xn%d" % i, [128, D]) for i in range(2)]
            jk, r_jk = SB(esD, "jk", [128, D], BF16)
            st2, r_st2 = SB(esD, "st2", [128, 4])
            for ti in range(16):
                mt, r_mt = mixt[ti % 2]
                x_, r_x = xr[ti % 2]
                xo, r_xo = xn_[ti % 2]
                DMA(mt[:], mixscr.ap()[:, :, ti * 128:(ti + 1) * 128].rearrange("k p t -> p k t"), [], [r_mt])
                DMA(x_[:], xres[ti * 128:(ti + 1) * 128, :], [], [r_x])
                for q in range(4):
                    ps, r_ps = PS()
                    for kc in range(16):
                        MM(ps[:, :], mt[:, kc, :], wob[:, kc, q * 512:(q + 1) * 512], [r_mt, r_wob], [r_ps], start=(kc == 0), stop=(kc == 15))
                    qs = slice(q * 512, (q + 1) * 512)
                    TT("dve", xo[:, qs], ps[:, :], gtbc[:, qs], ALU.mult, [r_ps, r_gtbc], [r_xo])
                TT("dve", xo[:, :], xo[:, :], x_[:, :], ALU.add, [r_xo, r_x], [r_xo])
                S.op("act", lambda g: g.activation(out=jk[:], in_=xo[:], func=AF.Square, accum_out=st2[:, 0:1]), [r_xo], [r_jk, r_st2])
                ACT(st2[:, 1:2], st2[:, 0:1], AF.Sqrt, [r_st2], [r_st2], bias=1e-6, scale=1.0 / D)
                S.op("dve", lambda g: g.reciprocal(out=st2[:, 2:3], in_=st2[:, 1:2]), [r_st2], [r_st2])
                STT(xo[:, :], xo[:, :], st2[:, 2:3], fg[:, :], ALU.mult, ALU.mult, [r_xo, r_st2, r_fg], [r_xo])
                DMA(out[ti * 128:(ti + 1) * 128, :], xo[:, :], [r_xo], [])
            for i in range(S.NDMA):
                if S.dcnt[i] > 0:
                    S._wait("sp", i, S.dcnt[i])
        print("ninst", S.ninst)
    return nc


def _pool_mats():
    m = np.zeros((4, 64, 64), np.float32)
    for g, win in enumerate((2, 4, 8, 16)):
        for i in range(64):
            lo = min(max(i - win // 2, 0), 63)
            hi = min(max(i + win // 2 - 1, 0), 63)
            m[g, i, lo:hi + 1] = 1.0 / (hi - lo + 1)
            m[g, i, i] -= 1.0
    out = np.zeros((128, 4, 128), np.float32)
    for g in range(4):
        for blk in range(2):
            out[blk * 64:(blk + 1) * 64, g, blk * 64:(blk + 1) * 64] = m[g].T
    return out


def _consts():
    c = np.zeros((128, 3392), np.float32)
    c[:, 0:128] = np.eye(128)
    c[0:64, 128:192] = 1.0
    c[64:128, 192:256] = 1.0
    c[0:64, 256:320] = np.eye(64)
    c[64:128, 256:320] = np.eye(64)
    p = np.arange(128)[:, None]
    f = np.arange(128)[None, :]
    mf = [f < p, f > p, f >= p, f > p, f >= p]
    mbk = [f > p, f < p, f <= p, f < p, f <= p]
    for j in range(5):
        c[:, 320 + 128 * j:320 + 128 * (j + 1)] = mf[j]
        c[:, 960 + 128 * j:960 + 128 * (j + 1)] = mbk[j]
    for lv in range(7):
        m = 1 << lv
        e = ((p // (2 * m)) == (f // (2 * m))) & ((p % (2 * m)) >= m) & ((f % (2 * m)) < m)
        c[:, 1600 + 128 * lv:1600 + 128 * (lv + 1)] = e
        c[:, 2496 + 128 * lv:2496 + 128 * (lv + 1)] = e.T
    return c


def _fm(v, n):
    return np.ascontiguousarray(np.asarray(v, np.float32).reshape(n, 128).T)


def make_in_maps(x, c, ctx, c_ctx, norm_g, w_ada, b_ada, w_in, shift_mu, w0, w_up, a0, a_up,
                 k_k, k_a, r_k, lnx_g, lnx_b, w_pool, pool_scale, w_out, final_g):
    x = np.asarray(x, np.float32)
    ctx = np.asarray(ctx, np.float32)
    shared = {
        "w_ada": np.ascontiguousarray(np.asarray(w_ada, np.float32)[0]),
        "w_in": np.ascontiguousarray(np.asarray(w_in, np.float32)[0]),
        "w_out": np.ascontiguousarray(np.asarray(w_out, np.float32)[0]),
        "bgate": np.ascontiguousarray(np.broadcast_to(np.asarray(b_ada, np.float32)[0, 4096:6144][None, :], (128, D))),
        "fgb": np.ascontiguousarray(np.broadcast_to(np.asarray(final_g, np.float32)[None, :], (128, D))),
        "wup": np.ascontiguousarray(np.asarray(w_up, np.float32)[0].reshape(128, 1024)),
        "aup": np.ascontiguousarray(np.asarray(a_up, np.float32)[0].reshape(128, 1024)),
        "wpool": np.ascontiguousarray(np.asarray(w_pool, np.float32)[0].reshape(4, 2, 128, 256).transpose(2, 0, 1, 3)),
        "mpoolT": _pool_mats(),
        "consts": _consts(),
    }
    vbase = np.zeros((128, NVEC), np.float32)

    def put(name, arr):
        o, w = VEC[name]
        vbase[:, o:o + w] = arr

    put("norm_g", _fm(np.asarray(norm_g)[0], 16))
    put("b_ada", _fm(np.asarray(b_ada)[0], 48))
    mu = np.asarray(shift_mu, np.float32)[0]
    put("mu", np.concatenate([_fm(mu[0], 26), _fm(mu[1], 26)], axis=1))
    w0_ = np.asarray(w0, np.float32)[0]
    a0_ = np.asarray(a0, np.float32)[0]
    put("w0", np.stack([_fm(w0_[0], 8), _fm(w0_[1], 8)], axis=2).reshape(128, 16))
    put("a0", np.stack([_fm(a0_[0], 8), _fm(a0_[1], 8)], axis=2).reshape(128, 16))
    put("k_k", _fm(np.asarray(k_k)[0], 8))
    put("k_a", _fm(np.asarray(k_a)[0], 8))
    put("r_k", _fm(np.asarray(r_k)[0].reshape(-1), 8))
    put("lnx_g", _fm(np.asarray(lnx_g)[0], 8))
    put("lnx_b", _fm(np.asarray(lnx_b)[0], 8))
    put("pool_scale", _fm(np.asarray(pool_scale)[0], 8))
    maps = []
    for core in range(8):
        b, j = core // 4, core % 4
        xi = np.zeros((NROW, D), np.float32)
        lo = j * SEG
        if j > 0:
            xi[0] = x[b, lo - 1]
        xi[1:1 + SEG] = x[b, lo:lo + SEG]
        if j < 3:
            xi[1 + SEG] = x[b, lo + SEG]
        xi[CTX0:CTX0 + CTX] = ctx[b]
        cm = np.zeros((NT,), np.float32)
        cm[1:1 + SEG] = 1.0
        cm[CTX0:CTX0 + CTX] = 1.0
        if j > 0:
            cm[0] = 1.0
        if j < 3:
            cm[1 + SEG] = 1.0
        cvec = np.stack([np.asarray(c, np.float32)[b], np.asarray(c_ctx, np.float32)], axis=0)
        cT = np.ascontiguousarray(cvec.reshape(2, 16, 128).transpose(2, 1, 0))
        v = vbase.copy()
        cmask = np.zeros((16,), np.float32)
        for r_ in range(8):
            rb, rj = r_ // 4, r_ % 4
            cmask[r_] = 1.0 if (rb == b and rj < j) else 0.0
            cmask[8 + r_] = 1.0 if (rb == b and rj > j) else 0.0
        o, w = VEC["cmask"]
        v[:, o:o + w] = cmask[None, :]
        m = dict(shared)
        m.update({"xin": xi, "xres": np.ascontiguousarray(x[b, lo:lo + SEG]), "cT": cT,
                  "colmask": np.ascontiguousarray(np.broadcast_to(cm[None, :], (128, NT))), "vecs": v})
        maps.append(m)
    return maps


_NC_CACHE = {}


def kernel(**inputs):
    if "nc" not in _NC_CACHE:
        _NC_CACHE["nc"] = build()
    nc = _NC_CACHE["nc"]
    maps = make_in_maps(**inputs)
    res = run_bass_kernel_spmd(nc, maps, core_ids=list(range(8)))
    outp = np.zeros((2, 4 * SEG, D), np.float32)
    for core in range(8):
        b, j = core // 4, core % 4
        outp[b, j * SEG:(j + 1) * SEG] = res.results[core]["out"]
    return outp
```
